# Optimizing a Trainium2 kernel written in Bass

```python
import jax, jax.numpy as jnp
from jax import lax
import numpy as np

D_MODEL = 1024
BATCH = 2
SEQ = 8192
DEPTH = 4

GRID_W = 64
CTX_LEN = 256
HEAD_DIM = 64
ATTN_WIDTH = D_MODEL // 2
N_Q_HEADS = ATTN_WIDTH // HEAD_DIM
N_KV_HEADS = 2
Q_GROUP = N_Q_HEADS // N_KV_HEADS
KV_WIDTH = N_KV_HEADS * HEAD_DIM
ATTN_SCALE = HEAD_DIM ** -0.5
Q_BLOCK = 128
ROPE_THETA = 10000.0
POOL_WINDOWS = (2, 4, 8, 16)
POOL_WIDTH = D_MODEL - ATTN_WIDTH
POOL_GROUP_DIM = POOL_WIDTH // len(POOL_WINDOWS)
IN_A_WIDTH = ATTN_WIDTH + 2 * KV_WIDTH + POOL_WIDTH
OUT_A_WIDTH = ATTN_WIDTH + POOL_WIDTH
CHUNK = 128
GMLP_WIDTH = D_MODEL
GMLP_GROUPS = 8
GMLP_GROUP_DIM = GMLP_WIDTH // GMLP_GROUPS
D_FF = 2816
N_MOD = 9
EPS = 1e-6
N_EVEN = (DEPTH + 1) // 2
N_ODD = DEPTH // 2

kernel_name = "hybrid_attn_pool_gmlp_dit_trunk"


def rms_norm(x, g):
    xf = x.astype(jnp.float32)
    y = xf * lax.rsqrt(jnp.mean(xf * xf, axis=-1, keepdims=True) + EPS)
    return (y * g.astype(jnp.float32)).astype(x.dtype)


def modulate(x, g, shift, scale):
    return rms_norm(x, g) * (1 + scale) + shift


def swiglu(h, w13, w2):
    a, b = jnp.split(h @ w13, 2, axis=-1)
    return (jax.nn.silu(a) * b) @ w2


def half_ffn(x, g, shift, scale, gate, w13, w2):
    return x + 0.5 * gate * swiglu(modulate(x, g, shift, scale), w13, w2)


def axial_rope_tables(n_tokens, dtype):
    n_rows = n_tokens // GRID_W
    rows = jnp.repeat(jnp.arange(n_rows), GRID_W).astype(jnp.float32)
    cols = jnp.tile(jnp.arange(GRID_W), n_rows).astype(jnp.float32)
    half = HEAD_DIM // 2
    inv_freq = ROPE_THETA ** (-jnp.arange(0, half, 2, dtype=jnp.float32) / half)
    ang_r = rows[:, None] * inv_freq[None, :]
    ang_c = cols[:, None] * inv_freq[None, :]
    return (jnp.cos(ang_r).astype(dtype), jnp.sin(ang_r).astype(dtype),
            jnp.cos(ang_c).astype(dtype), jnp.sin(ang_c).astype(dtype))


def rope_half(x, cos, sin):
    x1, x2 = jnp.split(x, 2, axis=-1)
    cos = cos[None, :, None, :]
    sin = sin[None, :, None, :]
    return jnp.concatenate([x1 * cos - x2 * sin, x1 * sin + x2 * cos], axis=-1)


def apply_axial_rope(x, tabs):
    cr, sr, cc, sc = tabs
    xr, xc = jnp.split(x, 2, axis=-1)
    return jnp.concatenate([rope_half(xr, cr, sr), rope_half(xc, cc, sc)], axis=-1)


def gqa_attend(q, k, v):
    b, nq = q.shape[:2]
    qg = q.reshape(b, nq, N_KV_HEADS, Q_GROUP, HEAD_DIM)
    sc = jnp.einsum('bqhgd,bkhd->bhgqk', qg, k, preferred_element_type=jnp.float32) * ATTN_SCALE
    pr = jax.nn.softmax(sc, axis=-1).astype(v.dtype)
    return jnp.einsum('bhgqk,bkhd->bqhgd', pr, v).reshape(b, nq, ATTN_WIDTH)


def latent_attention(q, k_lat, v_lat, k_ctx, v_ctx):
    b, s = q.shape[:2]
    k_all = jnp.concatenate([k_ctx, k_lat], axis=1)
    v_all = jnp.concatenate([v_ctx, v_lat], axis=1)
    nb = s // Q_BLOCK
    qb = jnp.moveaxis(q.reshape(b, nb, Q_BLOCK, N_Q_HEADS, HEAD_DIM), 1, 0)
    ob = lax.map(lambda qblk: gqa_attend(qblk, k_all, v_all), qb)
    return jnp.moveaxis(ob, 0, 1).reshape(b, s, ATTN_WIDTH)


def project_a(h, w_in, qk_g):
    b, n = h.shape[:2]
    z = h @ w_in
    q = rms_norm(z[..., :ATTN_WIDTH].reshape(b, n, N_Q_HEADS, HEAD_DIM), qk_g[0])
    k = rms_norm(z[..., ATTN_WIDTH:ATTN_WIDTH + KV_WIDTH].reshape(b, n, N_KV_HEADS, HEAD_DIM), qk_g[1])
    v = z[..., ATTN_WIDTH + KV_WIDTH:ATTN_WIDTH + 2 * KV_WIDTH].reshape(b, n, N_KV_HEADS, HEAD_DIM)
    p = z[..., ATTN_WIDTH + 2 * KV_WIDTH:]
    return q, k, v, p


def project_ctx_kv(h, w_in, qk_g):
    b, n = h.shape[:2]
    z = h @ w_in[:, ATTN_WIDTH:ATTN_WIDTH + 2 * KV_WIDTH]
    k = rms_norm(z[..., :KV_WIDTH].reshape(b, n, N_KV_HEADS, HEAD_DIM), qk_g[1])
    v = z[..., KV_WIDTH:].reshape(b, n, N_KV_HEADS, HEAD_DIM)
    return k, v


def multiscale_pool(p, pool_w, pool_scale):
    n = p.shape[1]
    pf = p.astype(jnp.float32)
    cs = jnp.concatenate([jnp.zeros_like(pf[:, :1]), jnp.cumsum(pf, axis=1)], axis=1)
    t = jnp.arange(n)
    outs = []
    for gi, w in enumerate(POOL_WINDOWS):
        sl = slice(gi * POOL_GROUP_DIM, (gi + 1) * POOL_GROUP_DIM)
        lo = jnp.clip(t - w // 2, 0, n)
        hi = jnp.clip(t + w // 2, 0, n)
        csg = cs[..., sl]
        mean = (csg[:, hi] - csg[:, lo]) / (hi - lo).astype(jnp.float32)[None, :, None]
        outs.append((mean - pf[..., sl]).astype(p.dtype) @ pool_w[gi])
    return jnp.concatenate(outs, axis=-1) * pool_scale


def combine_a(attn, p, pool_w, pool_scale, w_out):
    return jnp.concatenate([attn, multiscale_pool(p, pool_w, pool_scale)], axis=-1) @ w_out


def mixer_c(h, w_in, v_g, w_sp, b_sp, w_out):
    b, n = h.shape[:2]
    z = jax.nn.gelu(h @ w_in)
    u, v = jnp.split(z, 2, axis=-1)
    v = rms_norm(v, v_g)
    vc = v.reshape(b, n // CHUNK, CHUNK, GMLP_GROUPS, GMLP_GROUP_DIM)
    sv = jnp.einsum('gpq,bnqgc->bnpgc', w_sp, vc) + b_sp.T[None, None, :, :, None]
    return (u * sv.reshape(b, n, GMLP_WIDTH)) @ w_out


def setup_inputs(seed: int = 0) -> dict:
    key = jax.random.key(seed)
    ks = jax.random.split(key, 20)
    f32 = jnp.float32

    def nrm(k, shape, fan_in):
        return jax.random.normal(k, shape, f32) * (fan_in ** -0.5)

    def gain(k, shape):
        return 1.0 + 0.05 * jax.random.normal(k, shape, f32)

    return {
        "x": jax.random.normal(ks[0], (BATCH, SEQ, D_MODEL), f32),
        "c": jax.random.normal(ks[1], (BATCH, D_MODEL), f32),
        "ctx": jax.random.normal(ks[2], (BATCH, CTX_LEN, D_MODEL), f32),
        "c_ctx": jax.random.normal(ks[3], (D_MODEL,), f32),
        "w_mod": 0.5 * nrm(ks[4], (DEPTH, D_MODEL, N_MOD * D_MODEL), D_MODEL),
        "b_mod": 0.01 * jax.random.normal(ks[5], (DEPTH, N_MOD * D_MODEL), f32),
        "norm_g": gain(ks[6], (DEPTH, 3, D_MODEL)),
        "ffn_w13": nrm(ks[7], (DEPTH, 2, D_MODEL, 2 * D_FF), D_MODEL),
        "ffn_w2": nrm(ks[8], (DEPTH, 2, D_FF, D_MODEL), D_FF),
        "w_in_a": nrm(ks[9], (N_EVEN, D_MODEL, IN_A_WIDTH), D_MODEL),
        "qk_norm_g": gain(ks[10], (N_EVEN, 2, HEAD_DIM)),
        "pool_w": nrm(ks[11], (N_EVEN, len(POOL_WINDOWS), POOL_GROUP_DIM, POOL_GROUP_DIM), POOL_GROUP_DIM),
        "pool_scale": gain(ks[12], (N_EVEN, POOL_WIDTH)),
        "w_out_a": nrm(ks[13], (N_EVEN, OUT_A_WIDTH, D_MODEL), OUT_A_WIDTH),
        "w_in_c": nrm(ks[14], (N_ODD, D_MODEL, 2 * GMLP_WIDTH), D_MODEL),
        "v_norm_g": gain(ks[15], (N_ODD, GMLP_WIDTH)),
        "w_sp": nrm(ks[16], (N_ODD, GMLP_GROUPS, CHUNK, CHUNK), CHUNK),
        "b_sp": gain(ks[17], (N_ODD, GMLP_GROUPS, CHUNK)),
        "w_out_c": nrm(ks[18], (N_ODD, GMLP_WIDTH, D_MODEL), GMLP_WIDTH),
    }


def reference(x, c, ctx, c_ctx, w_mod, b_mod, norm_g, ffn_w13, ffn_w2,
              w_in_a, qk_norm_g, pool_w, pool_scale, w_out_a,
              w_in_c, v_norm_g, w_sp, b_sp, w_out_c):
    b, s, d = x.shape
    rope_tabs = axial_rope_tables(s, x.dtype)
    silu_c = jax.nn.silu(c)
    silu_cc = jax.nn.silu(c_ctx)
    cs = ctx
    for i in range(DEPTH):
        m = (silu_c @ w_mod[i] + b_mod[i]).reshape(b, 1, N_MOD, d)
        reads_ctx = (i % 2 == 0)
        later_reads_ctx = any(j % 2 == 0 for j in range(i + 1, DEPTH))
        ctx_needed = reads_ctx or later_reads_ctx
        if ctx_needed:
            mc = (silu_cc @ w_mod[i] + b_mod[i]).reshape(1, 1, N_MOD, d)
            cs = half_ffn(cs, norm_g[i, 0], mc[:, :, 0], mc[:, :, 1], mc[:, :, 2],
                          ffn_w13[i, 0], ffn_w2[i, 0])
        x = half_ffn(x, norm_g[i, 0], m[:, :, 0], m[:, :, 1], m[:, :, 2], ffn_w13[i, 0], ffn_w2[i, 0])

        h = modulate(x, norm_g[i, 1], m[:, :, 3], m[:, :, 4])
        if i % 2 == 0:
            e = i // 2
            hc = modulate(cs, norm_g[i, 1], mc[:, :, 3], mc[:, :, 4])
            if later_reads_ctx:
                qc, kc, vc, pc = project_a(hc, w_in_a[e], qk_norm_g[e])
            else:
                kc, vc = project_ctx_kv(hc, w_in_a[e], qk_norm_g[e])
            q, k, v, p = project_a(h, w_in_a[e], qk_norm_g[e])
            q = apply_axial_rope(q, rope_tabs)
            k = apply_axial_rope(k, rope_tabs)
            attn = latent_attention(q, k, v, kc, vc)
            x = x + m[:, :, 5] * combine_a(attn, p, pool_w[e], pool_scale[e], w_out_a[e])
            if later_reads_ctx:
                attn_c = gqa_attend(qc, kc, vc)
                cs = cs + mc[:, :, 5] * combine_a(attn_c, pc, pool_w[e], pool_scale[e], w_out_a[e])
        else:
            o = i // 2
            x = x + m[:, :, 5] * mixer_c(h, w_in_c[o], v_norm_g[o], w_sp[o], b_sp[o], w_out_c[o])
            if later_reads_ctx:
                hc = modulate(cs, norm_g[i, 1], mc[:, :, 3], mc[:, :, 4])
                cs = cs + mc[:, :, 5] * mixer_c(hc, w_in_c[o], v_norm_g[o], w_sp[o], b_sp[o], w_out_c[o])

        if later_reads_ctx:
            cs = half_ffn(cs, norm_g[i, 2], mc[:, :, 6], mc[:, :, 7], mc[:, :, 8],
                          ffn_w13[i, 1], ffn_w2[i, 1])
        x = half_ffn(x, norm_g[i, 2], m[:, :, 6], m[:, :, 7], m[:, :, 8], ffn_w13[i, 1], ffn_w2[i, 1])
    return x
```

```python
import contextlib
import os
import numpy as np
import concourse.bass as bass
import concourse.mybir as mybir
from concourse.bass_utils import run_bass_kernel_spmd

F32 = mybir.dt.float32
BF16 = mybir.dt.bfloat16
ALU = mybir.AluOpType
AF = mybir.ActivationFunctionType
AX = mybir.AxisListType

D = 1024
DC = 8
TL = 2048
TCX = 256
T = TL + TCX
DFF = 2816
NJ = 22
EPS = 1e-6
TBS = [(0, 512, 0), (512, 512, 0), (1024, 512, 0), (1536, 512, 0), (2048, 256, 1)]
GROUPS = [[0, 1, 2, 3], [4, 5, 6, 7]]
POOL_W = (2, 4, 8, 16)


class Op:
    __slots__ = ("eng", "fn", "deps", "dma", "dma_val", "sig", "sigval", "waits", "name")


class Sched:
    def __init__(self):
        self.ops = []
        self.last_writer = {}
        self.readers = {}
        self.dma_cum = {}
        self.fence_op = None
        self.last_eng = {}
        self.last_dma = {}

    def add(self, eng, fn, reads=(), writes=(), dma=None, name=None, nofence=False):
        op = Op()
        op.eng, op.fn, op.dma, op.name = eng, fn, dma, name
        op.sig = False
        op.sigval = None
        deps = {}
        for r in reads:
            w = self.last_writer.get(r)
            if w is not None:
                deps[id(w)] = w
        for r in writes:
            w = self.last_writer.get(r)
            if w is not None:
                deps[id(w)] = w
            for rd in self.readers.get(r, ()):
                deps[id(rd)] = rd
        if self.fence_op is not None and not nofence:
            deps[id(self.fence_op)] = self.fence_op
        op.deps = list(deps.values())
        for r in reads:
            self.readers.setdefault(r, []).append(op)
        for r in writes:
            self.last_writer[r] = op
            self.readers[r] = []
        if dma is not None:
            key, n, inc = dma
            self.dma_cum[key] = self.dma_cum.get(key, 0) + inc * n
            op.dma_val = self.dma_cum[key]
            self.last_dma[key] = op
        else:
            op.dma_val = None
            self.last_eng[eng] = op
        self.ops.append(op)
        return op

    def fence(self, fn):
        op = Op()
        op.eng, op.fn, op.dma, op.name = "dve", fn, None, "fence"
        op.sig = False
        op.sigval = None
        op.dma_val = None
        deps = {}
        for o in list(self.last_eng.values()) + list(self.last_dma.values()):
            deps[id(o)] = o
        if self.fence_op is not None:
            deps[id(self.fence_op)] = self.fence_op
        op.deps = list(deps.values())
        self.last_eng["dve"] = op
        self.ops.append(op)
        self.fence_op = op

    def finalize(self):
        def skip(d, op):
            return d.dma is None and op.dma is None and d.eng == "pe" and op.eng == "pe"
        for op in self.ops:
            for d in op.deps:
                if d.dma is None and not skip(d, op):
                    d.sig = True
        cnt = {}
        for op in self.ops:
            if op.dma is None and op.sig:
                cnt[op.eng] = cnt.get(op.eng, 0) + 1
                op.sigval = cnt[op.eng]
        waited = {}
        for op in self.ops:
            need = {}
            for d in op.deps:
                if d.dma is not None:
                    k, v = ("dma", d.dma[0]), d.dma_val
                else:
                    if skip(d, op):
                        continue
                    k, v = ("eng", d.eng), d.sigval
                if need.get(k, 0) < v:
                    need[k] = v
            w = waited.setdefault(op.eng, {})
            op.waits = []
            for k, v in need.items():
                if w.get(k, 0) < v:
                    w[k] = v
                    op.waits.append((k, v))
        keys = set()
        for op in self.ops:
            for (k, v) in op.waits:
                keys.add(k)
            if op.dma is not None:
                keys.add(("dma", op.dma[0]))
            elif op.sig:
                keys.add(("eng", op.eng))
        self.sem_keys = sorted(keys, key=str)

    def emit(self, nc, final_waits=()):
        self.finalize()
        with contextlib.ExitStack() as st:
            sems = {}
            for i, k in enumerate(self.sem_keys):
                sems[k] = st.enter_context(nc.semaphore("sem%d" % i))
            blk = st.enter_context(nc.Block())
            by_eng = {}
            for op in self.ops:
                by_eng.setdefault(op.eng, []).append(op)

            def run(eng_name, e):
                for op in by_eng.get(eng_name, ()):
                    for (k, v) in op.waits:
                        e.wait_ge(sems[k], v)
                    if op.dma is not None:
                        insts = op.fn(e)
                        assert len(insts) == op.dma[1], (op.name, len(insts), op.dma)
                        for ins in insts:
                            ins.then_inc(sems[("dma", op.dma[0])], op.dma[2])
                    else:
                        ins = op.fn(e)
                        if op.sig:
                            ins.then_inc(sems[("eng", eng_name)], 1)
                if eng_name == "sp":
                    for key in final_waits:
                        e.wait_ge(sems[("dma", key)], self.dma_cum[key])

            blk.sync(lambda e: run("sp", e))
            blk.tensor(lambda e: run("pe", e))
            blk.scalar(lambda e: run("act", e))
            blk.vector(lambda e: run("dve", e))
            blk.gpsimd(lambda e: run("pool", e))


class Arena:
    def __init__(self, nc):
        n = nc.sbuf_bytes_remaining // 4 - 16
        self.f32 = nc.alloc_sbuf_tensor("arena", [128, n], F32)
        self.bf = self.f32.bitcast(BF16)
        self.nbytes = n * 4
        self.ptr = 0

    def alloc(self, free_shape, dt):
        n = int(np.prod(free_shape))
        nb = n * (4 if dt == F32 else 2)
        off = self.ptr
        self.ptr = (off + nb + 63) // 64 * 64
        assert self.ptr <= self.nbytes, ("arena overflow", self.ptr, self.nbytes)
        if dt == F32:
            v = self.f32[:, off // 4: off // 4 + n]
        else:
            v = self.bf[:, off // 2: off // 2 + n]
        if len(free_shape) == 2:
            v = v.rearrange("p (a b) -> p a b", a=free_shape[0])
        elif len(free_shape) == 3:
            v = v.rearrange("p (a b c) -> p a b c", a=free_shape[0], b=free_shape[1])
        return v

    def mark(self):
        return self.ptr

    def release(self, m):
        self.ptr = m


def build(layers, stop_after=None, debug=False):
    nc = bass.Bass("TRN2", target_bir_lowering=False)
    S = Sched()
    dbg_names = []

    def dump(name, ap, reads, dt=F32):
        if not debug:
            return
        d = nc.dram_tensor("dbg_" + name, list(ap.shape), dt, kind="ExternalOutput").ap()
        dbg_names.append(name)
        S.add("sp", lambda e: [e.dma_start(out=d, in_=ap)], reads=reads, writes=["dbg_" + name], dma=("dbg", 1, 16))

    def din(name, shape, dt=F32):
        return nc.dram_tensor(name, shape, dt, kind="ExternalInput").ap()

    xin = din("xT", [128, DC, T])
    cT = din("cT", [128, DC, 2])
    cmat = din("cmat", [128, 3, 128])
    wmod = {li: din("w_mod%d" % li, [D, 9 * D]) for li in layers}
    bmodT = din("bmodT", [128, 4 * 72])
    normgT = din("normgT", [128, 4 * 24])
    w13L = {li: din("w13L%d" % li, [2, NJ, 128, 2048]) for li in layers}
    w2 = {li: din("w2_%d" % li, [2, DFF, D]) for li in layers}
    ev = [li for li in layers if li % 2 == 0]
    od = [li for li in layers if li % 2 == 1]
    winaL = {li // 2: din("winaL%d" % (li // 2), [10, 128, 1024]) for li in ev}
    qkg_d = din("qkg", [128, 4])
    poolw = {li // 2: din("poolw%d" % (li // 2), [4, 128, 128]) for li in ev}
    poolsc_d = din("poolsc", [128, 8])
    woaA = {li // 2: din("woaA%d" % (li // 2), [8, 128, 512]) for li in ev}
    woaP = {li // 2: din("woaP%d" % (li // 2), [8, 128, 512]) for li in ev}
    rope_d = din("rope", [128, 2, TL])
    poolc_d = din("poolc", [128, 136])
    wincU = {li // 2: din("wincU%d" % (li // 2), [8, 128, 1024]) for li in od}
    wincV = {li // 2: din("wincV%d" % (li // 2), [2, 128, 4096]) for li in od}
    vng_d = din("vng", [128, 16])
    wspT_d = {li // 2: din("wspT%d" % (li // 2), [128, 1024]) for li in od}
    bspB_d = {li // 2: din("bspB%d" % (li // 2), [128, 1024]) for li in od}
    woc = {li // 2: din("woc%d" % (li // 2), [8, 128, 1024]) for li in od}
    outT = nc.dram_tensor("outT", [128, DC, T], F32, kind="ExternalOutput").ap()
    kloc_d = nc.dram_tensor("kloc_d", [128, TL], BF16, kind="Internal").ap()
    kall_d = nc.dram_tensor("kall_d", [512, TL], BF16, kind="Internal").ap()
    vloc_d = nc.dram_tensor("vloc_d", [128, 3072], BF16, kind="Internal").ap()
    vall_d = nc.dram_tensor("vall_d", [512, 3072], BF16, kind="Internal").ap()
    hloc = nc.dram_tensor("hloc", [128, 64], F32, kind="Internal").ap()
    hall = nc.dram_tensor("hall", [512, 64], F32, kind="Internal").ap()

    def sb(name, shape, dt=F32):
        return nc.alloc_sbuf_tensor(name, shape, dt)

    xT = sb("xT_sb", [128, DC, T])
    ones_m = sb("ones_m", [128, 128], BF16)
    bd64 = sb("bd64", [128, 128], BF16)
    Rm = sb("Rm", [128, 128], BF16)
    ones_f = sb("ones_f", [128, 64])
    Sw = sb("Sw", [128, 128])
    epsT = sb("epsT", [128, 1])
    cTs = sb("cTs", [128, DC, 2])
    scb = sb("scb", [128, DC, 2], BF16)
    mvecs = [sb("mvec0", [128, 144]), sb("mvec1", [128, 144])]
    ders = [sb("der0", [128, 96]), sb("der1", [128, 96])]
    cur = [0]
    normg = sb("normg", [128, 96])
    bmod = sb("bmod", [128, 288])
    qkg = sb("qkg_sb", [128, 4])
    poolsc = sb("poolsc_sb", [128, 8])
    vng = sb("vng_sb", [128, 16])
    poolc = sb("poolc_sb", [128, 136])
    fdummy = sb("fdummy", [128, 8])
    pspair = [nc.alloc_psum_tensor("psp%d" % i, [128, 1024], F32) for i in range(4)]
    ps = []
    for i in range(4):
        ps.append(pspair[i][:, 0:512])
        ps.append(pspair[i][:, 512:1024])
    A = Arena(nc)

    def PS(b):
        return ("ps", b)

    def mm_group(out_ap, pairs):
        def fn(e):
            ins = None
            n = len(pairs)
            for i, (l, r) in enumerate(pairs):
                ins = e.matmul(out_ap, lhsT=l, rhs=r, start=(i == 0), stop=(i == n - 1))
            return ins
        return fn

    def fence():
        S.fence(lambda e: e.memset(fdummy[:], 0.0))

    def mv(k, dc, v):
        i = (k * 8 + dc) * 2 + v
        return mvecs[cur[0]][:, i:i + 1]

    def dr(s, kind, dc, v):
        i = ((s * 2 + kind) * 8 + dc) * 2 + v
        return ders[cur[0]][:, i:i + 1]

    def MK():
        return "mvec%d" % cur[0]

    def DK():
        return "der%d" % cur[0]

    S.add("sp", lambda e: [e.dma_start(out=xT[:, dc, :], in_=xin[:, dc, :]) for dc in range(DC)],
          writes=[("x", dc, bi) for dc in range(DC) for bi in range(5)], dma=("xin", DC, 16))
    S.add("sp", lambda e: [e.dma_start(out=cTs[:], in_=cT), e.dma_start(out=normg[:], in_=normgT),
                           e.dma_start(out=bmod[:], in_=bmodT), e.dma_start(out=qkg[:], in_=qkg_d),
                           e.dma_start(out=poolsc[:], in_=poolsc_d), e.dma_start(out=vng[:], in_=vng_d),
                           e.dma_start(out=poolc[:], in_=poolc_d)],
          writes=["cTs", "normg", "bmod", "qkg", "poolsc", "vng", "poolc"], dma=("cin", 7, 16))
    S.add("pool", lambda e: [e.dma_start(out=bd64[:], in_=cmat[:, 0, :]), e.dma_start(out=Rm[:], in_=cmat[:, 1, :])],
          writes=["bd64", "Rm"], dma=("cmat", 2, 16))
    S.add("sp", lambda e: [e.dma_start(out=Sw[:], in_=cmat[:, 2, :])], writes=["Sw"], dma=("swld", 1, 16))
    S.add("dve", lambda e: e.memset(ones_m[:], 1.0 / D), writes=["ones_m"])
    S.add("dve", lambda e: e.memset(ones_f[:], 1.0), writes=["ones_f"])
    S.add("dve", lambda e: e.memset(epsT[:], EPS), writes=["eps"])
    S.add("act", lambda e: e.activation(out=scb[:], in_=cTs[:], func=AF.Silu), reads=["cTs"], writes=["scb"])

    def modvec_steps(li, wmp):
        pb_ = li % 2
        mvec = mvecs[pb_]
        der = ders[pb_]
        mk, dk = "mvec%d" % pb_, "der%d" % pb_
        steps = []

        def panel_dma(pnl):
            par = pnl % 2
            src = wmod[li][:, pnl * 256:(pnl + 1) * 256].rearrange("(k p) c -> p k c", p=128)
            S.add("pool", lambda e, par=par, src=src: [e.dma_start(out=wmp[par], in_=src)],
                  writes=[("wmp", par)], dma=(("wmp", par), 1, 16))

        def panel(pnl):
            par = pnl % 2

            def fn(e, par=par, pnl=pnl):
                ins = None
                for cc in range(2):
                    col = (pnl * 2 + cc) * 2
                    for k in range(8):
                        ins = e.matmul(ps[7][:, col:col + 2], lhsT=wmp[par][:, k, cc * 128:(cc + 1) * 128],
                                       rhs=scb[:, k, :], start=(k == 0), stop=(k == 7))
                return ins
            S.add("pe", fn, reads=[("wmp", par), "scb"], writes=[PS(7)])
        for pnl in range(36):
            steps.append((lambda pnl=pnl: panel_dma(pnl), lambda pnl=pnl: panel(pnl)))

        def final(s_):
            c_lo, c_hi = 24 * s_, 24 * s_ + 24
            for v in range(2):
                S.add("dve", lambda e, v=v: e.tensor_tensor(
                    out=mvec[:].rearrange("p (c v) -> p c v", v=2)[:, c_lo:c_hi, v],
                    in0=ps[7][:, 0:144].rearrange("p (c v) -> p c v", v=2)[:, c_lo:c_hi, v],
                    in1=bmod[:, li * 72 + c_lo:li * 72 + c_hi], op=ALU.add),
                    reads=[PS(7), "bmod"], writes=[mk])
            if True:
                for v in range(2):
                    mview = mvec[:].rearrange("p (c v) -> p c v", v=2)
                    dview = der[:].rearrange("p (c v) -> p c v", v=2)
                    sc_ap = mview[:, (3 * s_ + 1) * 8:(3 * s_ + 1) * 8 + 8, v]
                    gt_ap = mview[:, (3 * s_ + 2) * 8:(3 * s_ + 2) * 8 + 8, v]
                    S.add("dve", lambda e, s_=s_, v=v, sc_ap=sc_ap, dview=dview: e.scalar_tensor_tensor(
                        out=dview[:, (s_ * 2) * 8:(s_ * 2) * 8 + 8, v], in0=sc_ap, scalar=1.0,
                        in1=normg[:, li * 24 + s_ * 8: li * 24 + s_ * 8 + 8], op0=ALU.add, op1=ALU.mult),
                        reads=[mk, "normg"], writes=[dk])
                    S.add("dve", lambda e, s_=s_, v=v, gt_ap=gt_ap, dview=dview: e.tensor_scalar(
                        out=dview[:, (s_ * 2 + 1) * 8:(s_ * 2 + 1) * 8 + 8, v], in0=gt_ap,
                        scalar1=(1.0 if s_ == 1 else 0.5), scalar2=None, op0=ALU.mult),
                        reads=[mk], writes=[dk])
        for s_ in range(3):
            steps.insert(12 * (s_ + 1) + s_, (lambda: None, lambda s_=s_: final(s_)))
        return steps

    def modvec_head(li):
        fence()
        m0 = A.mark()
        wmp = [A.alloc([8, 256], BF16) for _ in range(2)]
        for (sa_, sb_) in modvec_steps(li, wmp)[:13]:
            sa_()
            sb_()
        A.release(m0)

    def norm_modulate(s, tbs, hT, at=None, nsq=1, after_block=None):
        save = A.ptr
        if at is not None:
            A.ptr = at
        sqs = [A.alloc([8, 512], BF16) for _ in range(nsq)]
        rs = [A.alloc([512], F32) for _ in range(2)]
        tm = [A.alloc([512], F32) for _ in range(2)]
        if at is not None:
            A.ptr = save
        mk, dk = MK(), DK()

        def stage1a(bi, c0, n, v):
            sq = sqs[bi % nsq]
            sk = ("sq", bi % nsq)
            S.add("act", lambda e: e.activation(out=sq[:, :, :n], in_=xT[:, :, c0:c0 + n], func=AF.Square),
                  reads=[("x", dc, bi) for dc in range(DC)], writes=[sk])
            sbk = 6 + (bi % nsq)
            S.add("pe", mm_group(ps[sbk][:, :n], [(ones_m[:], sq[:, dc, :n]) for dc in range(DC)]),
                  reads=[sk, "ones_m"], writes=[PS(sbk)])

        def stage1b(bi, c0, n, v):
            par = bi % 2
            sbk = 6 + (bi % nsq)
            S.add("act", lambda e: e.activation(out=rs[par][:, :n], in_=ps[sbk][:, :n], func=AF.Ln, bias=epsT[:, 0:1], scale=1.0),
                  reads=[PS(sbk), "eps"], writes=[("rs", par)])
            S.add("act", lambda e: e.activation(out=rs[par][:, :n], in_=rs[par][:, :n], func=AF.Exp, scale=-0.5),
                  reads=[("rs", par)], writes=[("rs", par)])

        def stage2(bi, c0, n, v):
            par = bi % 2
            for dc in range(DC):
                tp = dc % 2
                gs_ap = dr(s, 0, dc, v)
                sh_ap = mv(3 * s, dc, v)
                S.add("dve", lambda e, dc=dc, tp=tp, gs_ap=gs_ap: e.scalar_tensor_tensor(
                    out=tm[tp][:, :n], in0=xT[:, dc, c0:c0 + n], scalar=gs_ap, in1=rs[par][:, :n],
                    op0=ALU.mult, op1=ALU.mult),
                    reads=[("x", dc, bi), ("rs", par), dk], writes=[("tm", tp)])
                if dc % 2 == 0:
                    S.add("act", lambda e, dc=dc, tp=tp, sh_ap=sh_ap: e.activation(
                        out=hT[:, dc, c0:c0 + n], in_=tm[tp][:, :n], func=AF.Identity, bias=sh_ap, scale=1.0),
                        reads=[("tm", tp), mk], writes=[("hT", dc, bi)])
                else:
                    S.add("dve", lambda e, dc=dc, tp=tp, sh_ap=sh_ap: e.tensor_scalar(
                        out=hT[:, dc, c0:c0 + n], in0=tm[tp][:, :n], scalar1=sh_ap, scalar2=None, op0=ALU.add),
                        reads=[("tm", tp), mk], writes=[("hT", dc, bi)])
        blocks = [(bi, c0, n, v) for bi, (c0, n, v) in tbs]
        nb = len(blocks)
        if nsq >= 2:
            stage1a(*blocks[0])
            if nb > 1:
                stage1a(*blocks[1])
            stage1b(*blocks[0])
            for i in range(nb):
                if i + 2 < nb:
                    stage1a(*blocks[i + 2])
                if i + 1 < nb:
                    stage1b(*blocks[i + 1])
                stage2(*blocks[i])
                if after_block is not None:
                    after_block(i)
        else:
            stage1a(*blocks[0])
            stage1b(*blocks[0])
            for i in range(nb):
                if i + 1 < nb:
                    stage1a(*blocks[i + 1])
                    stage1b(*blocks[i + 1])
                stage2(*blocks[i])

    def ffn(li, h, s, with_ctx, nxt=None):
        fence()
        m0 = A.mark()
        steps = []
        if nxt is not None:
            wmp_ = [A.alloc([8, 256], BF16) for _ in range(2)]
            steps = modvec_steps(abs(nxt) - 1, wmp_)
            if nxt < 0:
                steps = steps[13:]
        tbs = list(enumerate(TBS if with_ctx else TBS[:4]))
        hT = A.alloc([8, T], BF16)
        gT = A.alloc([6, T], BF16)
        w13p = [A.alloc([8, 256], BF16) for _ in range(3)]
        w2p = A.alloc([6, 1024], BF16)
        sa = [A.alloc([512], F32) for _ in range(2)]

        def load13(j):
            par = j % 3
            S.add("pool", lambda e, par=par, j=j: [e.dma_start(out=w13p[par].rearrange("p k c -> p (k c)"), in_=w13L[li][h, j])],
                  writes=[("w13p", par)], dma=(("w13p", par), 1, 16))
        load13(0)
        load13(1)
        load13(2)
        cnt = {"pc": 0}
        oc = 0

        def ab_block(j, jj, bi, c0, n):
            par = j % 3
            pp = cnt["pc"] % 2
            cnt["pc"] += 1
            S.add("pe", mm_group(ps[pp][:, :n], [(w13p[par][:, k, 0:128], hT[:, k, c0:c0 + n]) for k in range(8)]),
                  reads=[("w13p", par)] + [("hT", k, bi) for k in range(8)], writes=[PS(pp)])
            S.add("pe", mm_group(ps[2 + pp][:, :n], [(w13p[par][:, k, 128:256], hT[:, k, c0:c0 + n]) for k in range(8)]),
                  reads=[("w13p", par)] + [("hT", k, bi) for k in range(8)], writes=[PS(2 + pp)])
            S.add("act", lambda e: e.activation(out=sa[pp][:, :n], in_=ps[pp][:, :n], func=AF.Silu),
                  reads=[PS(pp)], writes=[("sa", pp)])
            S.add("dve", lambda e: e.tensor_tensor(
                out=gT[:, jj, c0:c0 + n], in0=sa[pp][:, :n], in1=ps[2 + pp][:, :n], op=ALU.mult),
                reads=[("sa", pp), PS(2 + pp)], writes=[("gT", jj, bi)])

        def run_steps(k_):
            pend = []
            for _ in range(k_):
                if steps:
                    a_, b_ = steps.pop(0)
                    a_()
                    pend.append(b_)
            return pend
        for gi_, (j0, nj) in enumerate([(0, 6), (6, 6), (12, 5), (17, 5)]):
            src2 = w2[li][h, j0 * 128:(j0 + nj) * 128, :].rearrange("(j p) c -> p j c", p=128)
            S.add("pool", lambda e, nj=nj, src2=src2: [e.dma_start(out=w2p[:, 0:nj, :], in_=src2)],
                  writes=["w2p"], dma=("w2p", 1, 16))
            jstart = 0
            if gi_ == 0:
                pend = run_steps(2)

                def pair_cb(i):
                    bi, (c0, n, v) = tbs[i]
                    ab_block(0, 0, bi, c0, n)
                    ab_block(1, 1, bi, c0, n)
                norm_modulate(s, tbs, hT, nsq=2, after_block=pair_cb)
                load13(3)
                load13(4)
                for b_ in pend:
                    b_()
                jstart = 2
            for jj in range(jstart, nj):
                j = j0 + jj
                pend = run_steps(2)
                for bi, (c0, n, v) in tbs:
                    ab_block(j, jj, bi, c0, n)
                if j + 3 < NJ:
                    load13(j + 3)
                for b_ in pend:
                    b_()
            for dc in range(DC):
                for bi, (c0, n, v) in tbs:
                    po = 4 + oc % 2
                    oc += 1
                    S.add("pe", mm_group(ps[po][:, :n], [(w2p[:, jj, dc * 128:(dc + 1) * 128], gT[:, jj, c0:c0 + n]) for jj in range(nj)]),
                          reads=["w2p"] + [("gT", jj, bi) for jj in range(nj)], writes=[PS(po)])
                    g_ap = dr(s, 1, dc, v)
                    S.add("dve", lambda e, po=po, n=n, dc=dc, c0=c0, g_ap=g_ap: e.scalar_tensor_tensor(
                        out=xT[:, dc, c0:c0 + n], in0=ps[po][:, :n], scalar=g_ap, in1=xT[:, dc, c0:c0 + n],
                        op0=ALU.mult, op1=ALU.add),
                        reads=[PS(po), ("x", dc, bi), DK()], writes=[("x", dc, bi)])
        while steps:
            a_, b_ = steps.pop(0)
            a_()
            b_()
        A.release(m0)

    def gelu(psb, n, out_ap, out_key, g1, g2, gk):
        S.add("act", lambda e: e.activation(out=g1[:, :n], in_=ps[psb][:, :n], func=AF.Square, scale=0.21145921),
              reads=[PS(psb)], writes=[("g1", gk)])
        S.add("dve", lambda e: e.scalar_tensor_tensor(out=g2[:, :n], in0=g1[:, :n], scalar=1.0, in1=ps[psb][:, :n],
                                                      op0=ALU.add, op1=ALU.mult),
              reads=[("g1", gk), PS(psb)], writes=[("g2", gk)])
        S.add("act", lambda e: e.activation(out=g1[:, :n], in_=g2[:, :n], func=AF.Sigmoid, scale=1.5957691216),
              reads=[("g2", gk)], writes=[("g1", gk)])
        S.add("dve", lambda e: e.tensor_tensor(out=out_ap, in0=g1[:, :n], in1=ps[psb][:, :n], op=ALU.mult),
              reads=[("g1", gk), PS(psb)], writes=[out_key])

    def mixer_c(li, with_ctx):
        o = li // 2
        s = 1
        fence()
        m0 = A.mark()
        tbs = list(enumerate(TBS if with_ctx else TBS[:4]))
        hT = A.alloc([8, T], BF16)
        uT = A.alloc([8, T], BF16)
        scr_off = A.ptr
        vtok = A.alloc([4, 1024], BF16)
        vg = [A.alloc([1024], F32) for _ in range(4)]
        wu = [A.alloc([8, 128], BF16) for _ in range(2)]
        wv = [A.alloc([8, 512], BF16) for _ in range(2)]
        wsp = A.alloc([1024], BF16)
        bsp = A.alloc([1024], F32)
        junk = A.alloc([1024], F32)
        ssq = A.alloc([16], F32)
        svt = A.alloc([512], F32)
        norm_modulate(s, tbs, hT, at=scr_off, nsq=2)
        S.add("pool", lambda e: [e.dma_start(out=wv[0].rearrange("p k c -> p (k c)"), in_=wincV[o][0]),
                                 e.dma_start(out=wv[1].rearrange("p k c -> p (k c)"), in_=wincV[o][1]),
                                 e.dma_start(out=wsp, in_=wspT_d[o])],
              writes=["wv", "wsp"], dma=("wvsp", 3, 16))
        S.add("sp", lambda e: [e.dma_start(out=bsp, in_=bspB_d[o])], writes=["bsp"], dma=("bsp", 1, 16))

        def loadu(cc):
            par = cc % 2
            S.add("pool", lambda e, par=par, cc=cc: [e.dma_start(out=wu[par].rearrange("p k c -> p (k c)"), in_=wincU[o][cc])],
                  writes=[("wu", par)], dma=(("wu", par), 1, 16))
        loadu(0)
        loadu(1)
        pc = 0
        for cc in range(8):
            par = cc % 2
            for bi, (c0, n, v) in tbs:
                pp = pc % 2
                pc += 1
                S.add("pe", mm_group(ps[pp][:, :n], [(wu[par][:, k, :], hT[:, k, c0:c0 + n]) for k in range(8)]),
                      reads=[("wu", par)] + [("hT", k, bi) for k in range(8)], writes=[PS(pp)])
                S.add("act", lambda e, pp=pp, n=n, cc=cc, c0=c0: e.activation(out=uT[:, cc, c0:c0 + n], in_=ps[pp][:, :n], func=AF.Gelu_apprx_tanh),
                      reads=[PS(pp)], writes=[("uT", cc, bi)])
            if cc + 2 < 8:
                loadu(cc + 2)
        for bi, (c0, n, v) in tbs:
            ntt = n // 128
            for tt in range(ntt):
                for half in range(2):
                    pp = 2 + half
                    S.add("pe", mm_group(ps[pp][:, :], [(hT[:, k, c0 + tt * 128:c0 + (tt + 1) * 128], wv[half][:, k, :]) for k in range(8)]),
                          reads=["wv"] + [("hT", k, bi) for k in range(8)], writes=[PS(pp)])
                    S.add("act", lambda e, pp=pp, tt=tt, half=half: e.activation(out=vg[tt][:, half * 512:(half + 1) * 512], in_=ps[pp][:, :],
                                                                              func=AF.Gelu_apprx_tanh),
                          reads=[PS(pp)], writes=[("vg", tt, half)])
                S.add("dve", lambda e, tt=tt: e.tensor_tensor(out=junk, in0=vg[tt], in1=vg[tt], op=ALU.mult),
                      reads=[("vg", tt, 0), ("vg", tt, 1)], writes=["junk"])
                S.add("dve", lambda e, tt=tt: e.reduce_sum(out=ssq[:, tt:tt + 1], in_=junk, axis=AX.X), reads=["junk"], writes=[("ssq", tt)])
            S.add("act", lambda e, ntt=ntt: e.activation(out=ssq[:, 4:4 + ntt], in_=ssq[:, 0:ntt], func=AF.Sqrt, bias=epsT[:, 0:1], scale=1.0 / D),
                  reads=[("ssq", tt) for tt in range(ntt)] + ["eps"], writes=["ssq1"])
            S.add("dve", lambda e, ntt=ntt: e.reciprocal(out=ssq[:, 8:8 + ntt], in_=ssq[:, 4:4 + ntt]), reads=["ssq1"], writes=["ssq2"])
            for tt in range(ntt):
                S.add("dve", lambda e, tt=tt: e.tensor_scalar(out=vtok[:, tt, :], in0=vg[tt], scalar1=ssq[:, 8 + tt:9 + tt],
                                                              scalar2=None, op0=ALU.mult),
                      reads=[("vg", tt, 0), ("vg", tt, 1), "ssq2"], writes=[("vtok", 0, tt)])
            for g in range(8):
                pb = 4 + g % 2

                def fn(e, pb=pb, g=g, ntt=ntt):
                    ins = None
                    for tt in range(ntt):
                        ins = e.matmul(ps[pb][:, tt * 128:(tt + 1) * 128], lhsT=vtok[:, tt, g * 128:(g + 1) * 128],
                                       rhs=wsp[:, g * 128:(g + 1) * 128], start=True, stop=True)
                    return ins
                S.add("pe", fn, reads=[("vtok", 0, tt) for tt in range(ntt)] + ["wsp"], writes=[PS(pb)])
                for tt in range(ntt):
                    S.add("dve", lambda e, pb=pb, g=g, tt=tt: e.scalar_tensor_tensor(
                        out=svt[:, tt * 128:(tt + 1) * 128], in0=ps[pb][:, tt * 128:(tt + 1) * 128],
                        scalar=vng[:, o * 8 + g:o * 8 + g + 1], in1=bsp[:, g * 128:(g + 1) * 128], op0=ALU.mult, op1=ALU.add),
                        reads=[PS(pb), "vng", "bsp"], writes=[("svt", tt)])
                S.add("dve", lambda e, g=g, c0=c0, n=n: e.tensor_tensor(out=uT[:, g, c0:c0 + n], in0=svt[:, :n], in1=uT[:, g, c0:c0 + n], op=ALU.mult),
                      reads=[("svt", tt) for tt in range(ntt)] + [("uT", g, bi)], writes=[("uT", g, bi)])
        wo = wu

        def loado(cc):
            par = cc % 2
            S.add("pool", lambda e, par=par, cc=cc: [e.dma_start(out=wo[par].rearrange("p k c -> p (k c)"), in_=woc[o][cc])],
                  writes=[("wu", par)], dma=(("wu", par), 1, 16))
        loado(0)
        loado(1)
        oc = 0
        for cc in range(8):
            par = cc % 2
            for bi, (c0, n, v) in tbs:
                po = 6 + oc % 2
                oc += 1
                S.add("pe", mm_group(ps[po][:, :n], [(wo[par][:, k, :], uT[:, k, c0:c0 + n]) for k in range(8)]),
                      reads=[("wu", par)] + [("uT", k, bi) for k in range(8)], writes=[PS(po)])
                g_ap = dr(s, 1, cc, v)
                S.add("dve", lambda e, po=po, n=n, cc=cc, c0=c0, g_ap=g_ap: e.scalar_tensor_tensor(
                    out=xT[:, cc, c0:c0 + n], in0=ps[po][:, :n], scalar=g_ap, in1=xT[:, cc, c0:c0 + n],
                    op0=ALU.mult, op1=ALU.add),
                    reads=[PS(po), ("x", cc, bi), DK()], writes=[("x", cc, bi)])
            if cc + 2 < 8:
                loado(cc + 2)
        A.release(m0)

    def mixer_a(li, ctx_mode):
        e_ = li // 2
        s = 1
        fence()
        m0 = A.mark()
        tbs_all = list(enumerate(TBS))
        tbs_lat = tbs_all[:4]
        tbs_q = tbs_all if ctx_mode == 2 else tbs_lat
        hT = A.alloc([8, T], BF16)
        pooled = A.alloc([4, T], BF16)
        win = [A.alloc([8, 128], BF16) for _ in range(2)]
        regB = A.mark()
        pe_l = A.alloc([4, TL + 16], F32)
        pe_c = A.alloc([4, TCX + 16], F32)
        wk_off = A.ptr
        wk = [A.alloc([TL + 16 + TCX + 16], F32) for _ in range(2)]
        hal = A.alloc([4, 64], F32)
        hsb = A.alloc([4, 16], F32)
        pwb = A.alloc([4, 128], BF16)
        etmp = A.alloc([4, 8], F32)
        norm_modulate(s, tbs_all, hT, at=wk_off)
        order = [6, 7, 8, 9, 0, 1, 2, 3, 4, 5]

        def loadw(idx):
            par = idx % 2
            ch = order[idx]
            S.add("pool", lambda e, par=par, ch=ch: [e.dma_start(out=win[par].rearrange("p k c -> p (k c)"), in_=winaL[e_][ch])],
                  writes=[("win", par)], dma=(("win", par), 1, 16))
        loadw(0)
        loadw(1)
        S.add("pool", lambda e: [e.dma_start(out=pwb[:, gi, :], in_=poolw[e_][gi]) for gi in range(4)],
              writes=["pwb"], dma=("pwb", 4, 16))
        S.add("dve", lambda e: e.memset(pe_l[:, :, 0:8], 0.0), writes=["pe_halo0"])
        S.add("dve", lambda e: e.memset(pe_c, 0.0), writes=[("pe_c", gi) for gi in range(4)])
        pc = 0
        for idx in range(4):
            gi = idx
            par = idx % 2
            for bi, (c0, n, v) in tbs_q:
                pp = pc % 2
                pc += 1
                S.add("pe", mm_group(ps[pp][:, :n], [(win[par][:, k, :], hT[:, k, c0:c0 + n]) for k in range(8)]),
                      reads=[("win", par)] + [("hT", k, bi) for k in range(8)], writes=[PS(pp)])
                if v == 0:
                    S.add("act", lambda e, pp=pp, gi=gi, c0=c0, n=n: e.activation(out=pe_l[:, gi, 8 + c0:8 + c0 + n], in_=ps[pp][:, :n], func=AF.Copy),
                          reads=[PS(pp)], writes=[("pe_l", gi, bi)])
                else:
                    S.add("act", lambda e, pp=pp, gi=gi, n=n: e.activation(out=pe_c[:, gi, 8:8 + n], in_=ps[pp][:, :n], func=AF.Copy),
                          reads=[PS(pp), ("pe_c", gi)], writes=[("pe_c", gi)])
            loadw(idx + 2)
        S.add("dve", lambda e: e.tensor_copy(out=hsb[:, :, 0:8], in_=pe_l[:, :, 8:16]),
              reads=[("pe_l", gi, 0) for gi in range(4)], writes=["hsb0"])
        S.add("dve", lambda e: e.tensor_copy(out=hsb[:, :, 8:16], in_=pe_l[:, :, TL:TL + 8]),
              reads=[("pe_l", gi, 3) for gi in range(4)], writes=["hsb1"])
        S.add("sp", lambda e: [e.dma_start(out=hloc, in_=hsb.rearrange("p a b -> p (a b)"))], reads=["hsb0", "hsb1"], writes=["hloc"], dma=("hloc", 1, 16))
        S.add("pool", lambda e: [e.collective_compute("AllGather", ALU.bypass, replica_groups=GROUPS, ins=[hloc], outs=[hall])],
              reads=["hloc"], writes=["hall"], dma=("ccH", 1, 1))
        S.add("sp", lambda e: [e.dma_start(out=hal, in_=hall.rearrange("(r p) c -> p r c", p=128))], reads=["hall"], writes=["hal"], dma=("hal", 1, 16))
        halv = hal.rearrange("p r (g c) -> p r g c", g=4)
        for side in range(2):
            dst = pe_l[:, :, 0:8] if side == 0 else pe_l[:, :, TL + 8:TL + 16]
            key = "pe_halo%d" % side
            for r in range(4):
                src = halv[:, r, :, 8:16] if side == 0 else halv[:, r, :, 0:8]
                sel = poolc[:, 128 + side * 4 + r:128 + side * 4 + r + 1]
                if r == 0:
                    S.add("dve", lambda e, dst=dst, src=src, sel=sel: e.tensor_scalar(out=dst, in0=src, scalar1=sel, scalar2=None, op0=ALU.mult),
                          reads=["hal", "poolc"], writes=[key])
                else:
                    S.add("dve", lambda e, dst=dst, src=src, sel=sel: e.scalar_tensor_tensor(out=dst, in0=src, scalar=sel, in1=dst, op0=ALU.mult, op1=ALU.add),
                          reads=["hal", "poolc", key], writes=[key])
        for gi, w in enumerate(POOL_W):
            for (pe_, n_, base, isctx) in ([(pe_l[:, gi, :], TL, 0, 0)] + ([(pe_c[:, gi, :], TCX, TL + 16, 1)] if ctx_mode == 2 else [])):
                W_ = n_ + 16
                rd = ([("pe_l", gi, b) for b in range(4)] + ["pe_halo0", "pe_halo1"]) if not isctx else [("pe_c", gi)]
                cur = pe_
                step = 1
                bufi = 0
                ln = W_
                while step < w:
                    nl = ln - step
                    dstb = wk[bufi][:, base:base + nl]
                    S.add("dve", lambda e, dstb=dstb, cur=cur, nl=nl, step=step: e.tensor_tensor(out=dstb, in0=cur[:, 0:nl], in1=cur[:, step:step + nl], op=ALU.add),
                          reads=rd + [("wk", 0), ("wk", 1)], writes=[("wk", bufi)])
                    cur = wk[bufi][:, base:base + W_]
                    ln = nl
                    step *= 2
                    bufi ^= 1
                off = 8 - w // 2
                cbase = n_ if not isctx else 0
                outcol = 0 if not isctx else TL
                S.add("dve", lambda e, cur=cur, off=off, n_=n_, w=w, pe_=pe_, gi=gi, outcol=outcol: e.scalar_tensor_tensor(
                    out=pooled[:, gi, outcol:outcol + n_], in0=cur[:, off:off + n_], scalar=1.0 / w, in1=pe_[:, 8:8 + n_],
                    op0=ALU.mult, op1=ALU.subtract),
                    reads=rd + [("wk", 0), ("wk", 1)], writes=[("pooled", gi, isctx)])
                for edge in range(2):
                    tcol = 0 if edge == 0 else n_ - 8
                    ci = (64 if isctx else 0) + (gi * 2 + edge) * 8
                    S.add("dve", lambda e, cur=cur, off=off, tcol=tcol, ci=ci, gi=gi: e.tensor_tensor(
                        out=etmp[:, gi, :], in0=cur[:, off + tcol:off + tcol + 8], in1=poolc[:, ci:ci + 8], op=ALU.mult),
                        reads=rd + [("wk", 0), ("wk", 1), "poolc"], writes=[("etmp", gi)])
                    S.add("dve", lambda e, tcol=tcol, pe_=pe_, gi=gi, outcol=outcol: e.tensor_tensor(
                        out=pooled[:, gi, outcol + tcol:outcol + tcol + 8], in0=etmp[:, gi, :], in1=pe_[:, 8 + tcol:8 + tcol + 8], op=ALU.subtract),
                        reads=[("etmp", gi), ("pooled", gi, isctx)] + rd, writes=[("pooled", gi, isctx)])
        for gi in range(4):
            for bi, (c0, n, v) in tbs_q:
                pp = pc % 2
                pc += 1
                S.add("pe", mm_group(ps[pp][:, :n], [(pwb[:, gi, :], pooled[:, gi, c0:c0 + n])]),
                      reads=["pwb", ("pooled", gi, v)], writes=[PS(pp)])
                S.add("act", lambda e, pp=pp, gi=gi, c0=c0, n=n: e.activation(out=pooled[:, gi, c0:c0 + n], in_=ps[pp][:, :n], func=AF.Identity,
                                                                              scale=poolsc[:, e_ * 4 + gi:e_ * 4 + gi + 1]),
                      reads=[PS(pp), "poolsc"], writes=[("pooled", gi, v), ("pooledF", gi, bi)])
        dump("pe_l", pe_l, [("pe_l", gi, b) for gi in range(4) for b in range(4)] + ["pe_halo0", "pe_halo1"])
        dump("pooled", pooled, [("pooledF", gi, b) for gi in range(4) for b in range(4)], BF16)
        fence()
        A.release(regB)
        qT = A.alloc([4, T], BF16)
        kctx = A.alloc([TCX], BF16)
        vctx = A.alloc([2, 192], BF16)
        regC = A.mark()
        ropet = A.alloc([2, TL], F32)
        kloc = A.alloc([TL], BF16)
        vloc = A.alloc([16, 192], BF16)
        sqq = [A.alloc([512], BF16) for _ in range(2)]
        qg = [A.alloc([512], F32) for _ in range(2)]
        qgb = [A.alloc([512], BF16) for _ in range(2)]
        rsq = [A.alloc([512], F32) for _ in range(2)]
        t1 = [A.alloc([512], F32) for _ in range(2)]
        t2 = [A.alloc([512], F32) for _ in range(2)]
        S.add("sp", lambda e: [e.dma_start(out=ropet, in_=rope_d)], writes=["rope"], dma=("rope", 1, 16))
        S.add("dve", lambda e: e.memset(vloc, 1.0), writes=["vloc_init"])
        S.add("dve", lambda e: e.memset(vctx, 1.0), writes=["vall_init"])
        qc = 0
        for idx in range(4, 9):
            ch = order[idx]
            par = idx % 2
            gcol = e_ * 2 + (0 if ch < 4 else 1)
            tbs_here = tbs_q if ch < 4 else tbs_all
            for bi, (c0, n, v) in tbs_here:
                pp = qc % 2
                qc += 1
                S.add("pe", mm_group(ps[pp][:, :n], [(win[par][:, k, :], hT[:, k, c0:c0 + n]) for k in range(8)]),
                      reads=[("win", par)] + [("hT", k, bi) for k in range(8)], writes=[PS(pp)])
                S.add("act", lambda e, pp=pp, n=n: e.activation(out=sqq[pp][:, :n], in_=ps[pp][:, :n], func=AF.Square),
                      reads=[PS(pp)], writes=[("sqq", pp)])
                S.add("act", lambda e, pp=pp, n=n, gcol=gcol: e.activation(out=qg[pp][:, :n], in_=ps[pp][:, :n], func=AF.Identity, scale=qkg[:, gcol:gcol + 1]),
                      reads=[PS(pp), "qkg"], writes=[("qg", pp)])
                S.add("pe", mm_group(ps[6][:, :n], [(bd64[:], sqq[pp][:, :n])]), reads=["bd64", ("sqq", pp)], writes=[PS(6)])
                S.add("act", lambda e, pp=pp, n=n: e.activation(out=rsq[pp][:, :n], in_=ps[6][:, :n], func=AF.Ln, bias=epsT[:, 0:1], scale=1.0),
                      reads=[PS(6), "eps"], writes=[("rsq", pp)])
                S.add("act", lambda e, pp=pp, n=n: e.activation(out=rsq[pp][:, :n], in_=rsq[pp][:, :n], func=AF.Exp, scale=-0.5),
                      reads=[("rsq", pp)], writes=[("rsq", pp)])
                if ch < 4:
                    dst = qT[:, ch, c0:c0 + n]
                    dkey = ("qT", ch, bi)
                elif v == 0:
                    dst = kloc[:, c0:c0 + n]
                    dkey = ("kloc", bi)
                else:
                    dst = kctx[:, 0:n]
                    dkey = "kctx"
                if v == 0:
                    S.add("dve", lambda e, pp=pp, n=n: e.tensor_copy(out=qgb[pp][:, :n], in_=qg[pp][:, :n]), reads=[("qg", pp)], writes=[("qgb", pp)])
                    S.add("pe", mm_group(ps[2 + pp][:, :n], [(Rm[:], qgb[pp][:, :n])]), reads=["Rm", ("qgb", pp)], writes=[PS(2 + pp)])
                    S.add("dve", lambda e, pp=pp, n=n, c0=c0: e.tensor_tensor(out=t1[pp][:, :n], in0=qg[pp][:, :n], in1=ropet[:, 0, c0:c0 + n], op=ALU.mult),
                          reads=[("qg", pp), "rope"], writes=[("t1", pp)])
                    S.add("dve", lambda e, pp=pp, n=n, c0=c0: e.tensor_tensor(out=t2[pp][:, :n], in0=ps[2 + pp][:, :n], in1=ropet[:, 1, c0:c0 + n], op=ALU.mult),
                          reads=[PS(2 + pp), "rope"], writes=[("t2", pp)])
                    S.add("dve", lambda e, pp=pp, n=n: e.tensor_tensor(out=t1[pp][:, :n], in0=t1[pp][:, :n], in1=t2[pp][:, :n], op=ALU.add),
                          reads=[("t1", pp), ("t2", pp)], writes=[("t1", pp)])
                    S.add("dve", lambda e, pp=pp, n=n, dst=dst: e.tensor_tensor(out=dst, in0=t1[pp][:, :n], in1=rsq[pp][:, :n], op=ALU.mult),
                          reads=[("t1", pp), ("rsq", pp)], writes=[dkey])
                else:
                    S.add("dve", lambda e, pp=pp, n=n, dst=dst: e.tensor_tensor(out=dst, in0=qg[pp][:, :n], in1=rsq[pp][:, :n], op=ALU.mult),
                          reads=[("qg", pp), ("rsq", pp)], writes=[dkey])
            if idx + 2 < 10:
                loadw(idx + 2)
        par = 9 % 2
        for tt in range(18):
            bi = tt // 4 if tt < 16 else 4
            pb = 4 + (tt // 4) % 2
            col = (tt % 4) * 128
            S.add("pe", mm_group(ps[pb][:, col:col + 128], [(hT[:, k, tt * 128:(tt + 1) * 128], win[par][:, k, :]) for k in range(8)]),
                  reads=[("win", par)] + [("hT", k, bi) for k in range(8)], writes=[PS(pb)])
            if tt < 16:
                dstv = vloc[:, tt, :]
                rdv = ["vloc_init"]
                wrv = [("vloc", tt)]
            else:
                dstv = vctx[:, tt - 16, :]
                rdv = ["vall_init"]
                wrv = [("vctx", tt - 16)]
            S.add("act", lambda e, pb=pb, col=col, dstv=dstv: e.activation(
                out=dstv.rearrange("p (g c) -> p g c", g=3)[:, 0:3:2, :],
                in_=ps[pb][:, col:col + 128].rearrange("p (g c) -> p g c", g=2), func=AF.Copy),
                reads=[PS(pb)] + rdv, writes=wrv)
        S.add("sp", lambda e: [e.dma_start(out=kloc_d, in_=kloc)],
              reads=[("kloc", b) for b in range(4)], writes=["kloc_d"], dma=("kloc_d", 1, 16))
        S.add("pool", lambda e: [e.collective_compute("AllGather", ALU.bypass, replica_groups=GROUPS, ins=[kloc_d], outs=[kall_d])],
              reads=["kloc_d"], writes=["kall_d"], dma=("ccK", 1, 1))
        S.add("sp", lambda e: [e.dma_start(out=vloc_d, in_=vloc.rearrange("p a b -> p (a b)"))],
              reads=[("vloc", tt) for tt in range(16)], writes=["vloc_d"], dma=("vloc_d", 1, 16))
        S.add("pool", lambda e: [e.collective_compute("AllGather", ALU.bypass, replica_groups=GROUPS, ins=[vloc_d], outs=[vall_d])],
              reads=["vloc_d"], writes=["vall_d"], dma=("ccV", 1, 1))
        dump("qT", qT, [("qT", c, b) for c in range(4) for b in range(4)], BF16)
        dump("kloc", kloc, [("kloc", b) for b in range(4)], BF16)
        dump("vloc", vloc, [("vloc", tt) for tt in range(16)], BF16)
        fence()
        A.release(regC)
        K_all = A.alloc([8192], BF16)
        V_all = A.alloc([64, 192], BF16)
        for r in range(4):
            S.add("sp", lambda e, r=r: [e.dma_start(out=K_all[:, r * TL:(r + 1) * TL], in_=kall_d[r * 128:(r + 1) * 128, :]),
                                        e.dma_start(out=V_all[:, r * 16:(r + 1) * 16, :].rearrange("p a b -> p (a b)"), in_=vall_d[r * 128:(r + 1) * 128, :])],
                  reads=["kall_d", "vall_d"], writes=[("K_lat", r), ("V_lat", r)], dma=(("kvld", r), 2, 16))
        attnT = hT[:, 0:4, :]
        woA = [win[i_][:, 0:4, :] for i_ in range(2)]
        woP = [win[i_][:, 4:8, :] for i_ in range(2)]
        Pb = [A.alloc([1024], BF16) for _ in range(3)]
        Rr = A.alloc([512], F32)
        bcs = A.alloc([512], F32)
        SCALE = 64 ** -0.5
        qblocks = [(qb * 128, 0) for qb in range(16)] + ([(TL, 1), (TL + 128, 1)] if ctx_mode == 2 else [])
        pending_norm = []
        for qi, (q0, isctx) in enumerate(qblocks):
            bi = q0 // 512 if not isctx else 4
            kcs = list(range(66)) if not isctx else [64, 65]
            nk = len(kcs)
            acc = [2 + (qi % 2), 2 + (qi % 2)]
            accb = [2 * acc[0], 2 * acc[1] + 1]

            def emit_s(i, q0=q0, bi=bi, kcs=kcs):
                kc = kcs[i]
                pj = i % 2
                krd = ("K_lat", kc // 16) if kc < 64 else "kctx"
                for g in range(2):
                    pr = slice(g * 64, (g + 1) * 64)
                    klhs = K_all[pr, kc * 128:(kc + 1) * 128] if kc < 64 else kctx[pr, (kc - 64) * 128:(kc - 63) * 128]
                    S.add("pe", mm_group(ps[2 * pj + g], [(klhs, qT[pr, :, q0:q0 + 128])]),
                          reads=[krd] + [("qT", c, bi) for c in range(4)], writes=[PS(2 * pj + g)])
                pbi = i % 3
                S.add("act", lambda e, pj=pj, pbi=pbi: e.activation(out=Pb[pbi], in_=pspair[pj][:, :], func=AF.Exp, scale=SCALE),
                      reads=[PS(2 * pj), PS(2 * pj + 1)], writes=[("Pb", pbi)])

            def emit_v(i, kcs=kcs, nk=nk, accb=accb):
                kc = kcs[i]
                pj = i % 2
                vrd = ("V_lat", kc // 16) if kc < 64 else ("vctx", kc - 64)
                for g in range(2):
                    vl = V_all[:, kc, g * 64:g * 64 + 128] if kc < 64 else vctx[:, kc - 64, g * 64:g * 64 + 128]
                    pbi = i % 3
                    S.add("pe", lambda e, vl=vl, pbi=pbi, i=i, g=g, nk=nk, accb=accb: e.matmul(
                        ps[accb[g]], lhsT=vl, rhs=Pb[pbi][:, g * 512:(g + 1) * 512], start=(i == 0), stop=(i == nk - 1)),
                        reads=[vrd, ("Pb", pbi)], writes=[PS(accb[g])])
            emit_s(0)
            if nk > 1:
                emit_s(1)
            for i in range(nk):
                if i + 2 < nk:
                    emit_s(i + 2)
                emit_v(i)
                if i == min(3, nk - 1) and pending_norm:
                    pending_norm.pop(0)()
            a0, a1 = accb
            S.add("dve", lambda e, a1=a1: e.reciprocal(out=Rr[0:64, :], in_=ps[a1][0:64, :]), reads=[PS(a1)], writes=["Rr0"])
            S.add("dve", lambda e, a0=a0: e.reciprocal(out=Rr[64:128, :], in_=ps[a0][64:128, :]), reads=[PS(a0)], writes=["Rr1"])

            def fin(a0=a0, a1=a1, q0=q0):
                S.add("pe", mm_group(ps[0], [(Sw[:], Rr)]), reads=["Sw", "Rr0", "Rr1"], writes=[PS(0)])
                S.add("act", lambda e: e.activation(out=bcs, in_=ps[0], func=AF.Copy), reads=[PS(0)], writes=["bcs"])
                S.add("dve", lambda e: e.tensor_tensor(
                    out=attnT[0:64, :, q0:q0 + 128], in0=ps[a0][0:64, :].rearrange("p (j q) -> p j q", j=4),
                    in1=bcs[0:64, :].rearrange("p (j q) -> p j q", j=4), op=ALU.mult),
                    reads=[PS(a0), "bcs"], writes=[("attnT", 0, q0)])
                S.add("dve", lambda e: e.tensor_tensor(
                    out=attnT[64:128, :, q0:q0 + 128], in0=ps[a1][64:128, :].rearrange("p (j q) -> p j q", j=4),
                    in1=bcs[64:128, :].rearrange("p (j q) -> p j q", j=4), op=ALU.mult),
                    reads=[PS(a1), "bcs"], writes=[("attnT", 1, q0)])
            pending_norm.append(fin)
        while pending_norm:
            pending_norm.pop(0)()
        dump("attnT", attnT, [("attnT", g, qb * 128) for g in range(2) for qb in range(16)], BF16)
        dump("K_all", K_all, [("K_lat", r) for r in range(4)], BF16)
        def loadwo(cc):
            par = cc % 2
            S.add("pool", lambda e, par=par, cc=cc: [e.dma_start(out=woA[par].rearrange("p k c -> p (k c)"), in_=woaA[e_][cc]),
                                                      e.dma_start(out=woP[par].rearrange("p k c -> p (k c)"), in_=woaP[e_][cc])],
                  writes=[("wo", par)], dma=(("woa", par), 2, 16))
        loadwo(0)
        loadwo(1)
        oc = 0
        for cc in range(8):
            par = cc % 2
            for bi, (c0, n, v) in tbs_q:
                po = oc % 2
                oc += 1
                pairs = [(woA[par][:, hh, :], attnT[:, hh, c0:c0 + n]) for hh in range(4)] + \
                        [(woP[par][:, gi, :], pooled[:, gi, c0:c0 + n]) for gi in range(4)]
                qbs = [c0 + i * 128 for i in range(n // 128)]
                S.add("pe", mm_group(ps[po][:, :n], pairs),
                      reads=[("wo", par)] + [("attnT", g, q0) for g in range(2) for q0 in qbs] + [("pooledF", gi, bi) for gi in range(4)],
                      writes=[PS(po)])
                g_ap = dr(s, 1, cc, v)
                S.add("dve", lambda e, po=po, n=n, cc=cc, c0=c0, g_ap=g_ap: e.scalar_tensor_tensor(
                    out=xT[:, cc, c0:c0 + n], in0=ps[po][:, :n], scalar=g_ap, in1=xT[:, cc, c0:c0 + n],
                    op0=ALU.mult, op1=ALU.add),
                    reads=[PS(po), ("x", cc, bi), DK()], writes=[("x", cc, bi)])
            if cc + 2 < 8:
                loadwo(cc + 2)
        A.release(m0)

    done = False
    for idx_, li in enumerate(layers):
        cur[0] = li % 2
        if idx_ == 0:
            modvec_head(li)
            ffn(li, 0, 0, with_ctx=(li <= 2), nxt=-(li + 1))
        else:
            ffn(li, 0, 0, with_ctx=(li <= 2))
        if stop_after == (li, "ffn1"):
            break
        if li % 2 == 0:
            mixer_a(li, ctx_mode=(2 if li == 0 else 1))
        else:
            mixer_c(li, with_ctx=(li <= 1))
        if stop_after == (li, "mix"):
            break
        ffn(li, 1, 2, with_ctx=(li <= 1), nxt=((layers[idx_ + 1] + 1) if idx_ + 1 < len(layers) else None))
    S.add("sp", lambda e: [e.dma_start(out=outT[:, dc, :], in_=xT[:, dc, :]) for dc in range(DC)],
          reads=[("x", dc, bi) for dc in range(DC) for bi in range(5)], writes=["outT"], dma=("out", DC, 16))
    S.emit(nc, final_waits=["out"] + (["dbg"] if dbg_names else []))
    return nc


def _rope_tables(core):
    r = core % 4
    t = np.arange(r * TL, (r + 1) * TL)
    rows = (t // 64).astype(np.float32)
    cols = (t % 64).astype(np.float32)
    half = 32
    inv_freq = (10000.0 ** (-np.arange(0, half, 2, dtype=np.float32) / half)).astype(np.float32)
    ang_r = rows[:, None] * inv_freq[None, :]
    ang_c = cols[:, None] * inv_freq[None, :]
    cos64 = np.concatenate([np.cos(ang_r), np.cos(ang_r), np.cos(ang_c), np.cos(ang_c)], axis=1)
    sin64 = np.concatenate([np.sin(ang_r), np.sin(ang_r), np.sin(ang_c), np.sin(ang_c)], axis=1)
    cosT = np.concatenate([cos64.T, cos64.T], axis=0)
    sinT = np.concatenate([sin64.T, sin64.T], axis=0)
    return np.ascontiguousarray(np.stack([cosT, sinT], axis=1).astype(np.float32))


def _const_mats():
    bd = np.zeros((128, 128), np.float32)
    bd[:64, :64] = 1.0 / 64
    bd[64:, 64:] = 1.0 / 64
    R = np.zeros((128, 128), np.float32)
    for base in range(0, 128, 32):
        for i in range(16):
            R[base + i + 16, base + i] = -1.0
            R[base + i, base + i + 16] = 1.0
    Sw = np.zeros((128, 128), np.float32)
    for m in range(128):
        Sw[(m + 64) % 128, m] = 1.0
    return np.ascontiguousarray(np.stack([bd, R, Sw], axis=1))


def _poolc(core):
    r = core % 4
    pc = np.ones((136,), np.float32)
    for isctx, n, first, last in ((0, TL, r == 0, r == 3), (1, TCX, True, True)):
        for gi, w in enumerate(POOL_W):
            for edge in range(2):
                vals = np.full((8,), 1.0 / w, np.float32)
                for i in range(8):
                    if edge == 0 and first:
                        t = i
                        cnt = min(t + w // 2, 10 ** 9) - max(t - w // 2, 0)
                        vals[i] = 1.0 / cnt
                    if edge == 1 and last:
                        t = n - 8 + i
                        cnt = min(t + w // 2, n) - (t - w // 2)
                        vals[i] = 1.0 / cnt
                o = (64 if isctx else 0) + (gi * 2 + edge) * 8
                pc[o:o + 8] = vals
    sel = np.zeros((8,), np.float32)
    if r > 0:
        sel[r - 1] = 1.0
    if r < 3:
        sel[4 + r + 1] = 1.0
    pc[128:136] = sel
    return np.ascontiguousarray(np.broadcast_to(pc[None, :], (128, 136)).astype(np.float32))


def _fm(v):
    v = np.asarray(v, np.float32)
    lead = v.shape[:-1]
    return np.moveaxis(v.reshape(*lead, -1, 128), -1, 0)


def prepare(inputs, layers=(0, 1, 2, 3)):
    f = lambda k: np.asarray(inputs[k], np.float32)
    x, c, ctx, c_ctx = f("x"), f("c"), f("ctx"), f("c_ctx")
    shared = {}
    shared["cmat"] = _const_mats()
    shared["w_mod"] = np.ascontiguousarray(f("w_mod"))
    bm = f("b_mod").reshape(4, 72, 128)
    shared["bmodT"] = np.ascontiguousarray(bm.transpose(2, 0, 1).reshape(128, 288))
    ng = f("norm_g").reshape(4, 3, 8, 128)
    shared["normgT"] = np.ascontiguousarray(ng.transpose(3, 0, 1, 2).reshape(128, 96))
    w13 = f("ffn_w13")
    a = w13[..., :DFF].reshape(4, 2, 8, 128, NJ, 128)
    b = w13[..., DFF:].reshape(4, 2, 8, 128, NJ, 128)
    ab = np.stack([a, b], axis=5)
    shared["w13L"] = np.ascontiguousarray(ab.transpose(0, 1, 4, 3, 2, 5, 6).reshape(4, 2, NJ, 128, 2048))
    shared["w2"] = np.ascontiguousarray(f("ffn_w2"))
    wi = f("w_in_a")
    qcols = wi[:, :, :512].reshape(2, 1024, 8, 64)
    chunks = []
    for cq in range(4):
        chunks.append(np.concatenate([qcols[:, :, cq], qcols[:, :, 4 + cq]], axis=-1))
    chunks.append(wi[:, :, 512:640])
    chunks.append(wi[:, :, 640:768])
    for gi in range(4):
        chunks.append(wi[:, :, 768 + gi * 128:768 + (gi + 1) * 128])
    wl = np.stack(chunks, axis=1)
    shared["winaL"] = np.ascontiguousarray(wl.reshape(2, 10, 8, 128, 128).transpose(0, 1, 3, 2, 4).reshape(2, 10, 128, 1024))
    qg = f("qk_norm_g")
    shared["qkg"] = np.ascontiguousarray(np.concatenate([qg, qg], axis=-1).transpose(2, 0, 1).reshape(128, 4))
    shared["poolw"] = np.ascontiguousarray(f("pool_w"))
    shared["poolsc"] = np.ascontiguousarray(f("pool_scale").reshape(2, 4, 128).transpose(2, 0, 1).reshape(128, 8))
    wo = f("w_out_a")
    woa = wo[:, :512].reshape(2, 2, 4, 64, 8, 128)
    shared["woaA"] = np.ascontiguousarray(woa.transpose(0, 4, 1, 3, 2, 5).reshape(2, 8, 128, 512))
    wop = wo[:, 512:].reshape(2, 4, 128, 8, 128)
    shared["woaP"] = np.ascontiguousarray(wop.transpose(0, 3, 2, 1, 4).reshape(2, 8, 128, 512))
    wc = f("w_in_c")
    wu = wc[:, :, :1024].reshape(2, 8, 128, 8, 128)
    shared["wincU"] = np.ascontiguousarray(wu.transpose(0, 3, 2, 1, 4).reshape(2, 8, 128, 1024))
    wv = wc[:, :, 1024:].reshape(2, 8, 128, 2, 512)
    shared["wincV"] = np.ascontiguousarray(wv.transpose(0, 3, 2, 1, 4).reshape(2, 2, 128, 4096))
    shared["vng"] = np.ascontiguousarray(f("v_norm_g").reshape(2, 8, 128).transpose(2, 0, 1).reshape(128, 16))
    wsp = f("w_sp")
    shared["wspT"] = np.ascontiguousarray(wsp.transpose(0, 3, 1, 2).reshape(2, 128, 1024))
    bsp = f("b_sp")
    shared["bspB"] = np.ascontiguousarray(np.broadcast_to(bsp.reshape(2, 1, 1024), (2, 128, 1024)))
    woc_ = f("w_out_c").reshape(2, 8, 128, 8, 128)
    shared["woc"] = np.ascontiguousarray(woc_.transpose(0, 3, 2, 1, 4).reshape(2, 8, 128, 1024))
    per = {}
    for k in ("cmat", "bmodT", "normgT", "qkg", "poolsc", "vng"):
        per[k] = shared[k]
    for li in layers:
        per["w_mod%d" % li] = np.ascontiguousarray(shared["w_mod"][li])
        per["w13L%d" % li] = np.ascontiguousarray(shared["w13L"][li])
        per["w2_%d" % li] = np.ascontiguousarray(shared["w2"][li])
        i2 = li // 2
        if li % 2 == 0:
            for k in ("winaL", "poolw", "woaA", "woaP"):
                per["%s%d" % (k, i2)] = np.ascontiguousarray(shared[k][i2])
        else:
            for k in ("wincU", "wincV", "wspT", "bspB", "woc"):
                per["%s%d" % (k, i2)] = np.ascontiguousarray(shared[k][i2])
    shared = per
    in_maps = []
    for core in range(8):
        b_, r = core // 4, core % 4
        xs = np.concatenate([x[b_, r * TL:(r + 1) * TL], ctx[b_]], axis=0)
        m = dict(shared)
        m["xT"] = np.ascontiguousarray(xs.T.reshape(8, 128, T).transpose(1, 0, 2))
        cc = np.stack([c[b_], c_ctx], axis=-1)
        m["cT"] = np.ascontiguousarray(cc.reshape(8, 128, 2).transpose(1, 0, 2))
        m["rope"] = _rope_tables(core)
        m["poolc"] = _poolc(core)
        in_maps.append(m)
    return in_maps


def _gather(results):
    out = np.empty((2, 8192, D), np.float32)
    for core in range(8):
        b_, r = core // 4, core % 4
        o = np.asarray(results[core]["outT"], np.float32)
        out[b_, r * TL:(r + 1) * TL] = o[:, :, :TL].transpose(2, 1, 0).reshape(TL, D)
    return out


_NC_CACHE = {}


def kernel(**inputs):
    in_maps = prepare(inputs)
    key = "full"
    if key not in _NC_CACHE:
        _NC_CACHE[key] = build([0, 1, 2, 3])
    nc = _NC_CACHE[key]
    res = run_bass_kernel_spmd(nc, in_maps, core_ids=list(range(8)))
    return _gather(res.results)
```
